# Optimizing a Trainium2 kernel written in Bass

```python
import math
import jax, jax.numpy as jnp
from jax import lax
import numpy as np

D_MODEL = 2048
BATCH = 2
SEQ = 8192
DEPTH = 2

HEAD_DIM = 128
N_HEADS = D_MODEL // HEAD_DIM
N_FOX_HEADS = N_HEADS // 2
N_NSA_HEADS = N_HEADS - N_FOX_HEADS
NSA_GROUP = 4
NSA_KV_HEADS = N_NSA_HEADS // NSA_GROUP
D_MIX = (N_FOX_HEADS + N_NSA_HEADS) * HEAD_DIM
CMP_BLOCK = 32
CMP_STRIDE = 16
SLC_BLOCK = 64
SLC_TOPK = 16
WINDOW = 512
Q_BLOCK = 128
D_FF = -(-8 * D_MODEL // (3 * 256)) * 256
RMS_EPS = 1e-6
NEG_INF = -1e30
FORCE_SCORE = 1e9

FOX_W = N_FOX_HEADS * HEAD_DIM
NSA_W = N_NSA_HEADS * HEAD_DIM
KV_W = NSA_KV_HEADS * HEAD_DIM
IN_SPLITS = (FOX_W, FOX_W, FOX_W, N_FOX_HEADS, NSA_W, KV_W, KV_W, KV_W, KV_W, KV_W, KV_W, 3 * N_NSA_HEADS)
IN_WIDTH = sum(IN_SPLITS)

kernel_name = "hybrid_fox_nsa_parallel_heads"


def _rms_norm(x, g):
    xf = x.astype(jnp.float32)
    y = xf * lax.rsqrt(jnp.mean(xf * xf, axis=-1, keepdims=True) + RMS_EPS)
    return (y * g.astype(jnp.float32)).astype(x.dtype)


def _masked_softmax(s, mask):
    s = jnp.where(mask, s, NEG_INF)
    m = jnp.max(s, axis=-1, keepdims=True)
    p = jnp.where(mask, jnp.exp(s - m), 0.0)
    return p / jnp.maximum(jnp.sum(p, axis=-1, keepdims=True), 1e-30)


def _alibi_slopes(n):
    return jnp.exp2(-8.0 * jnp.arange(1, n + 1, dtype=jnp.float32) / n)


def _overlap_matrix(n_cmp, n_slc):
    ci = np.arange(n_cmp)[:, None]
    sj = np.arange(n_slc)[None, :]
    ov = (ci * CMP_STRIDE < (sj + 1) * SLC_BLOCK) & (ci * CMP_STRIDE + CMP_BLOCK > sj * SLC_BLOCK)
    return ov.astype(np.float32)


def _compress(tok, pos_emb, w):
    B, T, G, Dh = tok.shape
    chunks = tok.reshape(B, T // CMP_STRIDE, CMP_STRIDE, G, Dh)
    blocks = jnp.concatenate([chunks[:, :-1], chunks[:, 1:]], axis=2)
    return jnp.einsum("bnlgd,lde->bnge", blocks + pos_emb[None, None, :, None, :], w)


def _hybrid_mixer(h, w_in, fox_forget_bias, fox_q_norm, fox_k_norm, nsa_q_norm, cmp_k_norm,
                  slc_k_norm, win_k_norm, cmp_pos_k, cmp_pos_v, cmp_w_k, cmp_w_v, w_out):
    B, T, _ = h.shape
    dt = h.dtype
    f32 = jnp.float32
    Hf, Hn, G, R, Dh = N_FOX_HEADS, N_NSA_HEADS, NSA_KV_HEADS, NSA_GROUP, HEAD_DIM
    scale = HEAD_DIM ** -0.5
    n_cmp = T // CMP_STRIDE - 1
    n_slc = T // SLC_BLOCK
    k_sel = min(SLC_TOPK, n_slc)
    n_qb = T // Q_BLOCK

    proj = h @ w_in
    cuts = [int(c) for c in np.cumsum(IN_SPLITS)[:-1]]
    qf, kf, vf, ff, qn, kc, vc, ks, vs, kw, vw, gt = jnp.split(proj, cuts, axis=-1)

    qf = _rms_norm(qf.reshape(B, T, Hf, Dh), fox_q_norm)
    kf = _rms_norm(kf.reshape(B, T, Hf, Dh), fox_k_norm)
    vf = vf.reshape(B, T, Hf, Dh)
    log_f = jax.nn.log_sigmoid(ff.astype(f32) + fox_forget_bias.astype(f32))
    cum_f = jnp.transpose(jnp.cumsum(log_f, axis=1), (0, 2, 1))

    qn = _rms_norm(qn.reshape(B, T, G, R, Dh), nsa_q_norm)
    k_cmp = _rms_norm(_compress(kc.reshape(B, T, G, Dh), cmp_pos_k, cmp_w_k), cmp_k_norm)
    v_cmp = _compress(vc.reshape(B, T, G, Dh), cmp_pos_v, cmp_w_v)
    cmp_end = jnp.arange(n_cmp) * CMP_STRIDE + (CMP_BLOCK - 1)
    k_slc = jnp.transpose(_rms_norm(ks.reshape(B, n_slc, SLC_BLOCK, G, Dh), slc_k_norm), (0, 3, 1, 2, 4))
    v_slc = jnp.transpose(vs.reshape(B, n_slc, SLC_BLOCK, G, Dh), (0, 3, 1, 2, 4))
    pad = ((0, 0), (WINDOW, 0), (0, 0), (0, 0))
    k_win = jnp.pad(_rms_norm(kw.reshape(B, T, G, Dh), win_k_norm), pad)
    v_win = jnp.pad(vw.reshape(B, T, G, Dh), pad)
    gates = jax.nn.sigmoid(gt.astype(f32)).astype(dt).reshape(B, T, G, R, 3)

    slopes = _alibi_slopes(Hn).reshape(G, R)
    overlap = jnp.asarray(_overlap_matrix(n_cmp, n_slc))
    key_pos = jnp.arange(T)
    blk_ids = jnp.arange(n_slc)
    bi = jnp.arange(B)[:, None, None, None]
    gi = jnp.arange(G)[None, :, None, None]
    win_off = jnp.arange(Q_BLOCK + WINDOW) - WINDOW
    slc_off = jnp.arange(SLC_BLOCK)

    def query_block(i):
        q0 = i * Q_BLOCK
        tpos = q0 + jnp.arange(Q_BLOCK)

        qf_b = lax.dynamic_slice_in_dim(qf, q0, Q_BLOCK, axis=1)
        cq = lax.dynamic_slice_in_dim(cum_f, q0, Q_BLOCK, axis=2)
        s = jnp.einsum("bqhd,bkhd->bhqk", qf_b, kf).astype(f32) * scale
        s = s + cq[..., None] - cum_f[:, :, None, :]
        p = _masked_softmax(s, key_pos[None, :] <= tpos[:, None])
        o_fox = jnp.einsum("bhqk,bkhd->bqhd", p.astype(dt), vf)

        qn_b = lax.dynamic_slice_in_dim(qn, q0, Q_BLOCK, axis=1)

        dist_c = (tpos[:, None] - cmp_end[None, :]).astype(f32)
        s_c = jnp.einsum("bqgrd,bngd->bgrqn", qn_b, k_cmp).astype(f32) * scale
        s_c = s_c - slopes[None, :, :, None, None] * dist_c
        p_c = _masked_softmax(s_c, dist_c >= 0)
        o_c = jnp.einsum("bgrqn,bngd->bqgrd", p_c.astype(dt), v_cmp)

        imp = jnp.einsum("bgrqn,nj->bgqj", p_c, overlap)
        cur = tpos // SLC_BLOCK
        forced = (blk_ids[None, :] == 0) | (blk_ids[None, :] == cur[:, None]) | (blk_ids[None, :] == cur[:, None] - 1)
        valid = blk_ids[None, :] * SLC_BLOCK <= tpos[:, None]
        imp = jnp.where(forced, FORCE_SCORE, jnp.where(valid, imp, -FORCE_SCORE))
        _, idx = lax.top_k(imp, k_sel)
        k_g = k_slc[bi, gi, idx]
        v_g = v_slc[bi, gi, idx]
        spos = idx[..., None] * SLC_BLOCK + slc_off
        dist_s = (tpos[None, None, :, None, None] - spos).astype(f32)[:, :, None]
        s_s = jnp.einsum("bqgrd,bgqkld->bgrqkl", qn_b, k_g).astype(f32) * scale
        s_s = s_s - slopes[None, :, :, None, None, None] * dist_s
        flat = (B, G, R, Q_BLOCK, k_sel * SLC_BLOCK)
        mask_s = jnp.broadcast_to(dist_s >= 0, s_s.shape).reshape(flat)
        p_s = _masked_softmax(s_s.reshape(flat), mask_s).reshape(s_s.shape)
        o_s = jnp.einsum("bgrqkl,bgqkld->bqgrd", p_s.astype(dt), v_g)

        k_wb = lax.dynamic_slice_in_dim(k_win, q0, Q_BLOCK + WINDOW, axis=1)
        v_wb = lax.dynamic_slice_in_dim(v_win, q0, Q_BLOCK + WINDOW, axis=1)
        wpos = q0 + win_off
        dist_w = tpos[:, None] - wpos[None, :]
        mask_w = (dist_w >= 0) & (dist_w < WINDOW) & (wpos[None, :] >= 0)
        s_w = jnp.einsum("bqgrd,bkgd->bgrqk", qn_b, k_wb).astype(f32) * scale
        s_w = s_w - slopes[None, :, :, None, None] * dist_w.astype(f32)
        p_w = _masked_softmax(s_w, mask_w)
        o_w = jnp.einsum("bgrqk,bkgd->bqgrd", p_w.astype(dt), v_wb)

        g_b = lax.dynamic_slice_in_dim(gates, q0, Q_BLOCK, axis=1)
        o_nsa = g_b[..., 0:1] * o_c + g_b[..., 1:2] * o_s + g_b[..., 2:3] * o_w
        return o_fox, o_nsa.reshape(B, Q_BLOCK, Hn, Dh)

    o_fox, o_nsa = lax.map(query_block, jnp.arange(n_qb))
    o_fox = jnp.moveaxis(o_fox, 0, 1).reshape(B, T, FOX_W)
    o_nsa = jnp.moveaxis(o_nsa, 0, 1).reshape(B, T, NSA_W)
    return jnp.concatenate([o_fox, o_nsa], axis=-1) @ w_out


def _swiglu(h, w_gate, w_up, w_down):
    return (jax.nn.silu(h @ w_gate) * (h @ w_up)) @ w_down


def setup_inputs(seed: int = 0) -> dict:
    key = jax.random.key(seed)
    ks = jax.random.split(key, 20)
    f32 = jnp.float32
    nrm = lambda k, shape, s: jax.random.normal(k, shape, f32) * s
    gain = lambda k, shape: 1.0 + 0.02 * jax.random.normal(k, shape, f32)
    L = DEPTH
    return {
        "x": nrm(ks[0], (BATCH, SEQ, D_MODEL), 1.0),
        "attn_norm": gain(ks[1], (L, D_MODEL)),
        "w_in": nrm(ks[2], (L, D_MODEL, IN_WIDTH), D_MODEL ** -0.5),
        "fox_forget_bias": 3.0 + 0.5 * jax.random.normal(ks[3], (L, N_FOX_HEADS), f32),
        "fox_q_norm": gain(ks[4], (L, HEAD_DIM)),
        "fox_k_norm": gain(ks[5], (L, HEAD_DIM)),
        "nsa_q_norm": gain(ks[6], (L, HEAD_DIM)),
        "cmp_k_norm": gain(ks[7], (L, HEAD_DIM)),
        "slc_k_norm": gain(ks[8], (L, HEAD_DIM)),
        "win_k_norm": gain(ks[9], (L, HEAD_DIM)),
        "cmp_pos_k": nrm(ks[10], (L, CMP_BLOCK, HEAD_DIM), 0.02),
        "cmp_pos_v": nrm(ks[11], (L, CMP_BLOCK, HEAD_DIM), 0.02),
        "cmp_w_k": nrm(ks[12], (L, CMP_BLOCK, HEAD_DIM, HEAD_DIM), (CMP_BLOCK * HEAD_DIM) ** -0.5),
        "cmp_w_v": nrm(ks[13], (L, CMP_BLOCK, HEAD_DIM, HEAD_DIM), (CMP_BLOCK * HEAD_DIM) ** -0.5),
        "w_out": nrm(ks[14], (L, D_MIX, D_MODEL), D_MIX ** -0.5),
        "ffn_norm": gain(ks[15], (L, D_MODEL)),
        "w_gate": nrm(ks[16], (L, D_MODEL, D_FF), D_MODEL ** -0.5),
        "w_up": nrm(ks[17], (L, D_MODEL, D_FF), D_MODEL ** -0.5),
        "w_down": nrm(ks[18], (L, D_FF, D_MODEL), D_FF ** -0.5),
    }


def reference(x, attn_norm, w_in, fox_forget_bias, fox_q_norm, fox_k_norm, nsa_q_norm, cmp_k_norm,
              slc_k_norm, win_k_norm, cmp_pos_k, cmp_pos_v, cmp_w_k, cmp_w_v, w_out, ffn_norm,
              w_gate, w_up, w_down):
    for l in range(DEPTH):
        h = _rms_norm(x, attn_norm[l])
        x = x + _hybrid_mixer(h, w_in[l], fox_forget_bias[l], fox_q_norm[l], fox_k_norm[l], nsa_q_norm[l],
                              cmp_k_norm[l], slc_k_norm[l], win_k_norm[l], cmp_pos_k[l], cmp_pos_v[l],
                              cmp_w_k[l], cmp_w_v[l], w_out[l])
        h = _rms_norm(x, ffn_norm[l])
        x = x + _swiglu(h, w_gate[l], w_up[l], w_down[l])
    return x
```

```python
import os
import numpy as np
from contextlib import ExitStack
import ml_dtypes
import concourse.bass as bass
import concourse.mybir as mybir
from concourse.bass_utils import run_bass_kernel_spmd

F32 = mybir.dt.float32
BF16 = mybir.dt.bfloat16
AF = mybir.ActivationFunctionType
ALU = mybir.AluOpType
NPBF = ml_dtypes.bfloat16

D = 2048
T = 8192
TO = 2048
DFF = 5632
INW = 5664
EPS = 1e-6
NEG = -30000.0
SCALE = 128 ** -0.5
C_QF, C_KF, C_VF, C_FF, C_QN, C_KC, C_VC, C_KS, C_VS, C_KW, C_VW, C_GT = 0, 1024, 2048, 3072, 3080, 4104, 4360, 4616, 4872, 5128, 5384, 5640


class Prog:
    ENG = ("pe", "act", "dve", "pool", "sp")

    def __init__(self, nc):
        self.nc = nc
        self.ops = []
        self.last_w = {}
        self.readers = {}
        self.barrier_ops = []

    def op(self, eng, fn, reads=(), writes=(), dma_key=None):
        idx = len(self.ops)
        xr = [r for r in reads if r.startswith("PS")]
        if xr:
            reads = [r for r in reads if not r.startswith("PS")]
            writes = list(writes) + xr
        deps = set(self.barrier_ops)
        for r in reads:
            if r in self.last_w:
                deps.add(self.last_w[r])
        for w in writes:
            if w in self.last_w:
                deps.add(self.last_w[w])
            for rd in self.readers.get(w, ()):
                deps.add(rd)
        self.ops.append([eng, fn, deps, dma_key])
        for r in reads:
            self.readers.setdefault(r, []).append(idx)
        for w in writes:
            self.last_w[w] = idx
            self.readers[w] = []
        return idx

    def barrier(self):
        last = {}
        for i, o in enumerate(self.ops):
            k = ("d", o[3]) if o[3] is not None else ("e", o[0])
            last[k] = i
        self.barrier_ops = list(last.values())
        self.last_w = {}
        self.readers = {}

    def emit(self):
        nc = self.nc
        ops = self.ops
        n = len(ops)

        def skip(d, i):
            return ops[d][3] is None and ops[d][0] == "pe" and ops[i][0] == "pe" and ops[i][3] is None

        needed = [False] * n
        for i, o in enumerate(ops):
            for d in o[2]:
                if not skip(d, i):
                    needed[d] = True
        sig = [None] * n
        ecnt = {e: 0 for e in self.ENG}
        dcnt = {}
        for i, (eng, fn, deps, dk) in enumerate(ops):
            if dk is not None:
                dcnt[dk] = dcnt.get(dk, 0) + 16
                sig[i] = (("d", dk), dcnt[dk])
            elif needed[i]:
                ecnt[eng] += 1
                sig[i] = (("e", eng), ecnt[eng])
        with ExitStack() as es:
            sems = {}
            for e in self.ENG:
                sems[("e", e)] = es.enter_context(nc.semaphore("s_" + e))
            for k in dcnt:
                sems[("d", k)] = es.enter_context(nc.semaphore("d_%d" % len(sems)))
            block = es.enter_context(nc.Block())
            per_eng = {e: [] for e in self.ENG}
            for i, o in enumerate(ops):
                per_eng[o[0]].append(i)

            def run(engobj, ename):
                seen = {}
                for i in per_eng[ename]:
                    eng, fn, deps, dk = ops[i]
                    waits = {}
                    for d in deps:
                        if sig[d] is None or skip(d, i):
                            continue
                        sk, v = sig[d]
                        if seen.get(sk, 0) >= v:
                            continue
                        waits[sk] = max(waits.get(sk, 0), v)
                    for sk, v in waits.items():
                        engobj.wait_ge(sems[sk], v)
                        seen[sk] = v
                    ins = fn(engobj)
                    if sig[i] is not None:
                        sk, v = sig[i]
                        ins.then_inc(sems[sk], 16 if sk[0] == "d" else 1)
                if ename == "sp":
                    for k, v in dcnt.items():
                        engobj.wait_ge(sems[("d", k)], v)

            @block.tensor
            def _(e):
                run(e, "pe")

            @block.scalar
            def _(e):
                run(e, "act")

            @block.vector
            def _(e):
                run(e, "dve")

            @block.gpsimd
            def _(e):
                run(e, "pool")

            @block.sync
            def _(e):
                run(e, "sp")
        return n, ecnt, len(dcnt)


class Ctx:
    def __init__(self, nc):
        self.nc = nc
        self.P = Prog(nc)
        self.es = ExitStack()
        self.rot = {}

    def sb(self, name, shape, dt):
        return self.es.enter_context(self.nc.sbuf_tensor(name, shape, dt))

    def din(self, name, shape, dt):
        return self.nc.dram_tensor(name, list(shape), dt, kind="ExternalInput").ap()

    def dout(self, name, shape, dt):
        return self.nc.dram_tensor(name, list(shape), dt, kind="ExternalOutput").ap()

    def nxt(self, key, n):
        v = self.rot.get(key, 0)
        self.rot[key] = v + 1
        return v % n


def build_A():
    nc = bass.Bass("TRN2", target_bir_lowering=False)
    C = Ctx(nc)
    P = C.P
    xT = C.din("xT", [D, TO], F32)
    w_in = C.din("w_in", [D, INW], F32)
    gcol_d = C.din("attn_g", [128, 16], F32)
    hg_d = C.din("head_g", [128, 8], F32)
    fb_d = C.din("fbias", [8, 1], F32)
    QF = C.dout("QF", [1024, TO], BF16)
    QN = C.dout("QN", [1024, TO], BF16)
    GF = C.dout("GF", [2048, TO], BF16)
    GT = C.dout("GT", [TO, 1536], BF16)
    GL = C.dout("GL", [8, TO], F32)
    GATES = C.dout("GATES", [24, TO], F32)
    with C.es:
        hT = C.sb("hT", [128, 16, TO], BF16)
        XC = [C.sb("XC%d" % k, [128, 16, 512], F32) for k in range(2)]
        SQ = [C.sb("SQ%d" % k, [128, 512], BF16) for k in range(2)]
        RS = [C.sb("RS%d" % k, [128, 512], F32) for k in range(2)]
        Y = [C.sb("Y%d" % k, [128, 512], F32) for k in range(2)]
        OB = [C.sb("OB%d" % k, [128, 512], BF16) for k in range(4)]
        OF = [C.sb("OF%d" % k, [128, 512], F32) for k in range(2)]
        W = [C.sb("W%d" % k, [128, 16, 512], BF16) for k in range(3)]
        gcol = C.sb("gcol", [128, 16], F32)
        hg = C.sb("hg", [128, 8], F32)
        fb = C.sb("fb", [8, 1], F32)
        ones = C.sb("ones", [128, 128], BF16)
        PS = [C.es.enter_context(nc.psum_tensor("PS%d" % k, [128, 512], F32)) for k in range(8)]

        P.op("sp", lambda e: e.dma_start(out=gcol[:], in_=gcol_d), writes=["gcol"], dma_key="gcol")
        P.op("sp", lambda e: e.dma_start(out=hg[:], in_=hg_d), writes=["hg"], dma_key="hg")
        P.op("sp", lambda e: e.dma_start(out=fb[:], in_=fb_d), writes=["fb"], dma_key="fb")
        P.op("dve", lambda e: e.memset(ones[:], 1.0), writes=["ones"])
        epsb = C.sb("epsb", [128, 1], F32)
        oneb = C.sb("oneb", [128, 1], F32)
        P.op("dve", lambda e: e.memset(epsb[:], EPS), writes=["epsb"])
        P.op("dve", lambda e: e.memset(oneb[:], 1.0), writes=["oneb"])
        P.op("dve", lambda e: e.tensor_scalar(out=hg[:, 0:1], in0=hg[:, 0:1], scalar1=SCALE, scalar2=None, op0=ALU.mult), reads=["hg"], writes=["hg"])
        P.op("dve", lambda e: e.tensor_scalar(out=hg[:, 2:3], in0=hg[:, 2:3], scalar1=SCALE, scalar2=None, op0=ALU.mult), reads=["hg"], writes=["hg"])
        P.op("dve", lambda e: e.tensor_scalar(out=fb[:], in0=fb[:], scalar1=-1.0, scalar2=None, op0=ALU.mult), reads=["fb"], writes=["fb"])

        xv = xT.rearrange("(dc p) t -> p dc t", p=128)
        for i in range(4):
            xc = XC[i % 2]; xk = "XC%d" % (i % 2)
            P.op("sp", lambda e, xc=xc, i=i: e.dma_start(out=xc[:], in_=xv[:, :, i * 512:(i + 1) * 512]), writes=[xk], dma_key=xk)
            ssp = PS[i % 2]; sk = "PS%d" % (i % 2)
            for dc in range(16):
                s = C.nxt("sq", 2)
                P.op("act", lambda e, s=s, xc=xc, dc=dc: e.activation(out=SQ[s][:], in_=xc[:, dc, :], func=AF.Square), reads=[xk], writes=["SQ%d" % s])
                P.op("pe", lambda e, s=s, ssp=ssp, dc=dc: e.matmul(ssp[:], lhsT=ones[:], rhs=SQ[s][:], start=(dc == 0), stop=(dc == 15)), reads=["SQ%d" % s, "ones"], writes=[sk])
            r = i % 2
            P.op("act", lambda e, r=r, ssp=ssp: e.activation(out=RS[r][:], in_=ssp[:], func=AF.Sqrt, bias=epsb[:, 0:1], scale=1.0 / D), reads=[sk, "epsb"], writes=["RS%d" % r])
            P.op("dve", lambda e, r=r: e.reciprocal(out=RS[r][:], in_=RS[r][:]), reads=["RS%d" % r], writes=["RS%d" % r])
            for dc in range(16):
                P.op("dve", lambda e, r=r, xc=xc, dc=dc, i=i: e.scalar_tensor_tensor(out=hT[:, dc, i * 512:(i + 1) * 512], in0=xc[:, dc, :], scalar=gcol[:, dc:dc + 1], in1=RS[r][:], op0=ALU.mult, op1=ALU.mult),
                     reads=[xk, "RS%d" % r, "gcol"], writes=["hT%d" % i])

        wv = w_in.rearrange("(dc p) c -> p dc c", p=128)

        def load_w(c0, width):
            s = C.nxt("w", 3)
            P.op("pool", lambda e, s=s: e.dma_start(out=W[s][:, :, 0:width], in_=wv[:, :, c0:c0 + width]), writes=["W%d" % s], dma_key="W%d" % s)
            return s

        def store(dst_ap, src_ap, key):
            P.op("sp", lambda e: e.dma_start(out=dst_ap, in_=src_ap), reads=[key], dma_key="st_" + key)

        def fm_block(ws, cb, i, kind, dst, gidx=None):
            p = C.nxt("ps", 6); pk = "PS%d" % p
            for dc in range(16):
                P.op("pe", lambda e, dc=dc: e.matmul(PS[p][:], lhsT=W[ws][:, dc, cb * 128:(cb + 1) * 128], rhs=hT[:, dc, i * 512:(i + 1) * 512], start=(dc == 0), stop=(dc == 15)),
                     reads=["W%d" % ws, "hT%d" % i], writes=[pk])
            o = C.nxt("ob", 4); ok = "OB%d" % o
            if kind == "plain":
                P.op("dve", lambda e: e.tensor_copy(out=OB[o][:], in_=PS[p][:]), reads=[pk], writes=[ok])
            else:
                y = C.nxt("y", 2); s = C.nxt("sq", 2); r = C.nxt("rs", 2)
                p2 = 6 + C.nxt("ps2", 2); p2k = "PS%d" % p2
                P.op("dve", lambda e: e.tensor_copy(out=Y[y][:], in_=PS[p][:]), reads=[pk], writes=["Y%d" % y])
                P.op("act", lambda e: e.activation(out=SQ[s][:], in_=Y[y][:], func=AF.Square), reads=["Y%d" % y], writes=["SQ%d" % s])
                P.op("pe", lambda e: e.matmul(PS[p2][:], lhsT=ones[:], rhs=SQ[s][:], start=True, stop=True), reads=["SQ%d" % s, "ones"], writes=[p2k])
                P.op("act", lambda e: e.activation(out=RS[r][:], in_=PS[p2][:], func=AF.Sqrt, bias=epsb[:, 0:1], scale=1.0 / 128), reads=[p2k, "epsb"], writes=["RS%d" % r])
                P.op("dve", lambda e: e.reciprocal(out=RS[r][:], in_=RS[r][:]), reads=["RS%d" % r], writes=["RS%d" % r])
                P.op("dve", lambda e: e.scalar_tensor_tensor(out=OB[o][:], in0=Y[y][:], scalar=hg[:, gidx:gidx + 1], in1=RS[r][:], op0=ALU.mult, op1=ALU.mult),
                     reads=["Y%d" % y, "RS%d" % r, "hg"], writes=[ok])
            store(dst, OB[o][:], ok)

        def tm_block(ws, tb, c_lo, width, dst):
            p = C.nxt("ps", 6); pk = "PS%d" % p
            i = tb // 4
            for dc in range(16):
                P.op("pe", lambda e, dc=dc: e.matmul(PS[p][:, 0:width], lhsT=hT[:, dc, tb * 128:(tb + 1) * 128], rhs=W[ws][:, dc, c_lo:c_lo + width], start=(dc == 0), stop=(dc == 15)),
                     reads=["W%d" % ws, "hT%d" % i], writes=[pk])
            o = C.nxt("ob", 4); ok = "OB%d" % o
            P.op("dve", lambda e: e.tensor_copy(out=OB[o][:, 0:width], in_=PS[p][:, 0:width]), reads=[pk], writes=[ok])
            store(dst, OB[o][:, 0:width], ok)


        KSTOP = int(os.environ.get("KSTOP", "99"))
        for (c0, dst, rbase, gidx) in ((C_QF, QF, 0, 0), (C_QF + 512, QF, 512, 0), (C_KF, GF, 0, 1), (C_KF + 512, GF, 512, 1),
                                       (C_QN, QN, 0, 2), (C_QN + 512, QN, 512, 2)):
            ws = load_w(c0, 512)
            for cb in range(4):
                for i in range(4):
                    fm_block(ws, cb, i, "norm", dst[rbase + cb * 128: rbase + (cb + 1) * 128, i * 512:(i + 1) * 512], gidx)
        for (c0, frow, gidx) in ((C_KS, 1024, 3), (C_KW, 1280, 4)):
            ws = load_w(c0, 256)
            for cb in range(2):
                for i in range(4):
                    fm_block(ws, cb, i, "norm", GF[frow + cb * 128: frow + (cb + 1) * 128, i * 512:(i + 1) * 512], gidx)
        for half in range(2):
            ws = load_w(C_VF + half * 512, 512)
            for tb in range(16):
                tm_block(ws, tb, 0, 512, GT[tb * 128:(tb + 1) * 128, half * 512:(half + 1) * 512])
        for (c0, tcol) in ((C_VS, 1024), (C_VW, 1280)):
            ws = load_w(c0, 256)
            for tb in range(16):
                tm_block(ws, tb, 0, 256, GT[tb * 128:(tb + 1) * 128, tcol:tcol + 256])
        ws = load_w(C_KC, 512)
        for cb in range(4):
            for i in range(4):
                fm_block(ws, cb, i, "plain", GF[1536 + cb * 128:1536 + (cb + 1) * 128, i * 512:(i + 1) * 512])
        ws = load_w(C_FF, 512)
        for i in range(4):
            p = C.nxt("ps", 6); pk = "PS%d" % p
            for dc in range(16):
                P.op("pe", lambda e, dc=dc, p=p, i=i, ws=ws: e.matmul(PS[p][0:8, :], lhsT=W[ws][:, dc, 0:8], rhs=hT[:, dc, i * 512:(i + 1) * 512], start=(dc == 0), stop=(dc == 15)),
                     reads=["W%d" % ws, "hT%d" % i], writes=[pk])
            o = C.nxt("of", 2); ok = "OF%d" % o
            P.op("act", lambda e, p=p, o=o: e.activation(out=OF[o][0:8, :], in_=PS[p][0:8, :], func=AF.Exp, bias=fb[:, 0:1], scale=-1.0), reads=[pk, "fb"], writes=[ok])
            P.op("act", lambda e, o=o: e.activation(out=OF[o][0:8, :], in_=OF[o][0:8, :], func=AF.Ln, bias=oneb[0:8, 0:1], scale=1.0), reads=[ok, "oneb"], writes=[ok])
            store(GL[:, i * 512:(i + 1) * 512], OF[o][0:8, :], ok)
        ws = load_w(INW - 512, 512)
        for i in range(4):
            p = C.nxt("ps", 6); pk = "PS%d" % p
            for dc in range(16):
                P.op("pe", lambda e, dc=dc, p=p, i=i, ws=ws: e.matmul(PS[p][0:24, :], lhsT=W[ws][:, dc, 488:512], rhs=hT[:, dc, i * 512:(i + 1) * 512], start=(dc == 0), stop=(dc == 15)),
                     reads=["W%d" % ws, "hT%d" % i], writes=[pk])
            o = C.nxt("of", 2); ok = "OF%d" % o
            P.op("act", lambda e, p=p, o=o: e.activation(out=OF[o][0:24, :], in_=PS[p][0:24, :], func=AF.Sigmoid), reads=[pk], writes=[ok])
            store(GATES[:, i * 512:(i + 1) * 512], OF[o][0:24, :], ok)
        print("A ops", P.emit())
    return nc


SLOPES = [2.0 ** (-(s + 1)) for s in range(8)]


def cb_of(kb):
    return ((kb % 16) // 4) * 16 + (kb // 16) * 4 + kb % 4


def _split3(v):
    v = v.astype(np.float32)
    hi = v.astype(NPBF)
    r1 = v - hi.astype(np.float32)
    mid = r1.astype(NPBF)
    lo = (r1 - mid.astype(np.float32)).astype(NPBF)
    return hi, mid, lo


_TAB = {}


def tables(j):
    if j in _TAB:
        return _TAB[j]
    kp = np.arange(128)[:, None]
    qp = np.arange(512)[None, :]
    t = {}
    MZ = np.zeros((128, 16, 512), np.float32)
    for z in range(16):
        MZ[:, z, :] = np.where(128 * z + kp <= 512 * j + qp, 0.0, NEG)
    t["MZ"] = MZ.astype(NPBF)
    MW = np.zeros((128, 20, 512), np.float32)
    for z in range(20):
        dist = (512 * j + qp) - (128 * (z - 4) + kp)
        MW[:, z, :] = np.where((dist >= 0) & (dist < 512), 0.0, NEG)
    t["MW"] = MW.astype(NPBF)
    t["MC"] = np.where(16 * kp + 31 <= 512 * j + qp, 0.0, NEG).astype(NPBF)
    t["MC2"] = np.where(16 * kp + 31 - 2048 <= 512 * j + qp, 0.0, NEG).astype(NPBF)
    RA = np.zeros((96, 512), NPBF)
    for s in range(8):
        for i in range(4):
            v = -SLOPES[s] * (512.0 * (4 * i + j) + np.arange(512))
            hi, mid, lo = _split3(v)
            m = s * 4 + i
            RA[m], RA[32 + m], RA[64 + m] = hi, mid, lo
    t["RA"] = RA
    SEL3 = np.zeros((96, 32, 128), np.float32)
    for m in range(32):
        SEL3[m, m, :] = 1; SEL3[32 + m, m, :] = 1; SEL3[64 + m, m, :] = 1
    t["SEL3"] = SEL3.astype(NPBF)
    KBA = np.zeros((128, 8, 64), np.float32)
    for kb in range(64):
        for s in range(8):
            KBA[:, s, cb_of(kb)] = SLOPES[s] * (128.0 * kb + np.arange(128))
    t["KBA"] = KBA
    KBC = np.zeros((128, 8, 4), np.float32)
    for nb in range(4):
        for s in range(8):
            KBC[:, s, nb] = SLOPES[s] * (16.0 * (128 * nb + np.arange(128)) + 31)
    t["KBC"] = KBC
    E = np.zeros((128, 64, 128), np.float32)
    for kb in range(64):
        for k in range(128):
            E[2 * kb + k // 64, cb_of(kb), k] = 1
    t["E"] = E.astype(NPBF)
    n = np.arange(512)[:, None]
    sj = np.arange(128)[None, :]
    ov = ((n * 16 < (sj + 1) * 64) & (n * 16 + 32 > sj * 64) & (n < 511)).astype(np.float32)
    t["OV"] = np.ascontiguousarray(ov.reshape(4, 128, 128).transpose(1, 0, 2))
    FVM = np.zeros((128, 16, 128), np.float32)
    FVA = np.zeros((128, 16, 128), np.float32)
    jj = np.arange(128)[None, :]
    for i in range(4):
        for u in range(4):
            tpos = 512 * (4 * i + j) + 128 * u + np.arange(128)[:, None]
            cur = tpos // 64
            valid = jj * 64 <= tpos
            f0 = (jj == 0); f1 = (jj == cur); f2 = (jj == cur - 1) & (cur - 1 > 0)
            forced = f0 | f1 | f2
            add = np.where(~valid, -1e9, 0.0)
            add = np.where(f0, 1e9, add); add = np.where(f2, 3e9, add); add = np.where(f1 & ~f0, 2e9, add)
            FVM[:, i * 4 + u, :] = (valid & ~forced).astype(np.float32)
            FVA[:, i * 4 + u, :] = add
    t["FVM"] = FVM; t["FVA"] = FVA
    TRI = np.zeros((128, 128), np.float32)
    for ps in range(128):
        for pd in range(128):
            js, hs, is_ = ps // 32, (ps % 32) // 4, ps % 4
            jd, hd, id_ = pd // 32, (pd % 32) // 4, pd % 4
            if hs == hd and 4 * is_ + js < 4 * id_ + jd:
                TRI[ps, pd] = 1
    t["TRI"] = TRI
    SELP = np.zeros((128, 32), np.float32)
    for m in range(32):
        SELP[j * 32 + m, m] = 1
    t["SELP"] = SELP
    SELG = np.zeros((64, 24, 128), np.float32)
    for c in range(24):
        SELG[c, c, :] = 1; SELG[32 + c, c, :] = 1
    t["SELG"] = SELG.astype(NPBF)
    t["IDF"] = np.eye(128, dtype=np.float32)
    t["IDB"] = np.eye(128, dtype=np.float32).astype(NPBF)
    _TAB[j] = t
    return t


TAB_SPECS = [("MZ", [128, 16, 512], BF16), ("MW", [128, 20, 512], BF16), ("MC", [128, 512], BF16), ("MC2", [128, 512], BF16), ("RA", [96, 512], BF16),
             ("SEL3", [96, 32, 128], BF16), ("KBA", [128, 8, 64], F32), ("KBC", [128, 8, 4], F32), ("E", [128, 64, 128], BF16),
             ("OV", [128, 4, 128], F32), ("FVM", [128, 16, 128], F32), ("FVA", [128, 16, 128], F32), ("TRI", [128, 128], F32),
             ("SELP", [128, 32], F32), ("SELG", [64, 24, 128], BF16), ("IDF", [128, 128], F32), ("IDB", [128, 128], BF16)]


def build_B(parts=("fox", "nsa")):
    nc = bass.Bass("TRN2", target_bir_lowering=False)
    C = Ctx(nc)
    P = C.P
    QF = C.din("QF", [1024, TO], BF16)
    QN = C.din("QN", [1024, TO], BF16)
    GATES = C.din("GATES", [24, TO], F32)
    GFa = C.din("GFa", [4, 2048, TO], BF16)
    GTa = C.din("GTa", [4, TO, 1536], BF16)
    GLa = C.din("GLa", [4, 8, TO], F32)
    cwk = C.din("cwk", [32, 128, 128], F32)
    cwv = C.din("cwv", [32, 128, 128], F32)
    cpk = C.din("cpkT", [128, 32], F32)
    cpv = C.din("cpvT", [128, 32], F32)
    ckn = C.din("ckn", [128, 1], F32)
    TD = {nm: C.din("T_" + nm, shp, dt) for nm, shp, dt in TAB_SPECS}
    OT = C.dout("OT", [2048, TO], BF16)
    with C.es:
        TB = {}
        for nm, shp, dt in TAB_SPECS:
            if nm in ("FVM", "FVA"):
                continue
            TB[nm] = C.sb("tb_" + nm, shp, dt)
            P.op("sp", lambda e, nm=nm: e.dma_start(out=TB[nm][:], in_=TD[nm]), writes=["tb_" + nm], dma_key="tb_" + nm)
        MZ, MW, MC, MC2, RA, SEL3, KBA, KBC, E, OV, TRI, SELP, SELG, IDF, IDB = [TB[nm] for nm, _, _ in TAB_SPECS if nm not in ('FVM', 'FVA')]
        PS = [C.es.enter_context(nc.psum_tensor("PS%d" % k, [128, 512], F32)) for k in range(8)]
        KV = [C.sb("KV%d" % k, [128, 8192], BF16) for k in range(4)]
        QT = [C.sb("QT%d" % k, [128, TO], BF16) for k in range(2)]
        PT = [C.sb("PT%d" % k, [128, 512], BF16) for k in range(3)]
        PC = [C.sb("PC%d" % k, [128, 512], F32) for k in range(4)]
        ACC = C.sb("ACC", [128, 4, 512], F32)
        IMP = C.sb("IMP", [128, 4, 128], F32)
        NST = C.sb("NST", [128, 512], BF16)
        RD = C.sb("RD", [128, 512], F32)
        FT = C.sb("FT", [128, 512], F32)
        TMP = C.sb("TMP", [128, 512], F32)
        OS = [C.sb("OS%d" % k, [128, 512], BF16) for k in range(2)]
        RF = C.sb("RF", [96, 512], BF16)
        DK = C.sb("DK", [128, 4, 128], F32)
        GS = KV[3][0:24, :].bitcast(F32)[:, 0:TO]
        GH = C.sb("GH", [64, TO], BF16)
        KCN = C.sb("KCN", [128, 2, 512], BF16)
        VCM = C.sb("VCM", [128, 2, 4, 128], F32)
        ones_b = C.sb("ones_b", [128, 128], BF16)
        ones_f = C.sb("ones_f", [128, 128], F32)
        epsb = C.sb("epsb", [128, 1], F32)
        SM = C.sb("SM", [128, 64], F32)
        W2 = C.sb("W2", [128, 128], F32)
        W3 = C.sb("W3", [128, 128], F32)
        FVMs = C.sb("FVMs", [128, 4, 128], F32)
        FVAs = C.sb("FVAs", [128, 4, 128], F32)
        P.op("dve", lambda e: e.memset(ones_b[:], 1.0), writes=["ones_b"])
        P.op("dve", lambda e: e.memset(ones_f[:], 1.0), writes=["ones_f"])
        P.op("dve", lambda e: e.memset(epsb[:], EPS), writes=["epsb"])
        P.op("dve", lambda e: e.memset(GH[:], 0.0), writes=["GH"])

        def mm(out, lhsT, rhs, start, stop, reads, writes):
            P.op("pe", lambda e: e.matmul(out, lhsT=lhsT, rhs=rhs, start=start, stop=stop), reads=reads, writes=writes)

        P.op("sp", lambda e: e.dma_start(out=GS, in_=GATES), writes=["KV3"], dma_key="KV3")
        P.op("dve", lambda e: e.tensor_copy(out=GH[0:24, :], in_=GS), reads=["KV3"], writes=["GH"])
        P.op("dve", lambda e: e.tensor_tensor(out=GS, in0=GS, in1=GH[0:24, :], op=ALU.subtract), reads=["GH"], writes=["KV3"])
        GLO = KV[2][0:24, 0:TO]
        P.op("dve", lambda e: e.tensor_copy(out=GLO, in_=GS), reads=["KV3"], writes=["KV2"])
        P.op("sp", lambda e: e.dma_start(out=GH[32:56, :], in_=GLO), reads=["KV2"], writes=["GH"], dma_key="GH")

        if "fox" in parts:
            LA, LB = PC[0], PC[1]
            P.op("sp", lambda e: e.dma_start(out=LA[:], in_=GLa.rearrange("j h (i q) -> (j h i) q", q=512)), writes=["PC0"], dma_key="PC0")
            cur, oth = (LA, "PC0"), (LB, "PC1")
            sft = 1
            while sft < 512:
                (a, ak), (b, bk) = cur, oth
                P.op("dve", lambda e, a=a, b=b, sft=sft: e.tensor_tensor(out=b[:, sft:512], in0=a[:, sft:512], in1=a[:, 0:512 - sft], op=ALU.add), reads=[ak], writes=[bk])
                P.op("dve", lambda e, a=a, b=b, sft=sft: e.tensor_copy(out=b[:, 0:sft], in_=a[:, 0:sft]), reads=[ak], writes=[bk])
                cur, oth = oth, cur
                sft *= 2
            Dt, Dk_ = cur
            mm(PS[6][:, 0:1], TRI[:], Dt[:, 511:512], True, True, [Dk_, "tb_TRI"], ["PS6"])
            P.op("dve", lambda e: e.tensor_copy(out=SM[:, 0:1], in_=PS[6][:, 0:1]), reads=["PS6"], writes=["SM"])
            P.op("dve", lambda e: e.tensor_scalar(out=Dt[:], in0=Dt[:], scalar1=SM[:, 0:1], scalar2=None, op0=ALU.add), reads=["SM", Dk_], writes=[Dk_])
            for u in range(4):
                P.op("pe", lambda e, u=u: e.transpose(PS[7][:, u * 128:(u + 1) * 128], Dt[:, u * 128:(u + 1) * 128], IDF[:]), reads=[Dk_, "tb_IDF"], writes=["PS7"])
            P.op("dve", lambda e: e.tensor_copy(out=DK[:].rearrange("p u k -> p (u k)"), in_=PS[7][:]), reads=["PS7"], writes=["DK"])
            mm(PS[6][0:32, :], SELP[:], Dt[:], True, True, [Dk_, "tb_SELP"], ["PS6"])
            T1 = PC[2]
            P.op("dve", lambda e: e.tensor_scalar(out=T1[0:32, :], in0=PS[6][0:32, :], scalar1=-1.0, scalar2=None, op0=ALU.mult), reads=["PS6"], writes=["PC2"])
            for k_ in range(3):
                P.op("dve", lambda e, k_=k_: e.tensor_copy(out=PT[k_][0:32, :], in_=T1[0:32, :]), reads=["PC2"], writes=["PT%d" % k_])
                if k_ < 2:
                    P.op("dve", lambda e, k_=k_: e.tensor_tensor(out=T1[0:32, :], in0=T1[0:32, :], in1=PT[k_][0:32, :], op=ALU.subtract), reads=["PT%d" % k_], writes=["PC2"])
                P.op("sp", lambda e, k_=k_: e.dma_start(out=RF[32 * k_:32 * k_ + 32, :], in_=PT[k_][0:32, :]), reads=["PT%d" % k_], writes=["RF"], dma_key="RF")

        if "nsa" in parts:
            KCt = KV[0]
            if True:
                WCf = KV[1][:].bitcast(F32).rearrange("p (l e) -> p l e", e=128)
                WCb = KV[2][:, 0:4096].rearrange("p (l e) -> p l e", e=128)
                cp = C.sb("cp", [128, 32], F32)
                YC = PC[3]
                SQc = PT[0]
                gk = C.sb("gk", [128, 1], F32)
                P.op("sp", lambda e: e.dma_start(out=gk[:], in_=ckn), writes=["gk"], dma_key="gk")
                for kind, wd, pd, row0 in (("k", cwk, cpk, 1536), ("v", cwv, cpv, 1792)):
                    P.op("sp", lambda e, wd=wd: e.dma_start(out=WCf, in_=wd.rearrange("l d e -> d l e")), writes=["KV1"], dma_key="KV1")
                    P.op("sp", lambda e, pd=pd: e.dma_start(out=cp[:], in_=pd), writes=["cp"], dma_key="cp")
                    P.op("dve", lambda e: e.tensor_copy(out=WCb, in_=WCf), reads=["KV1"], writes=["KV2"])
                    for l in range(32):
                        mm(PS[7][:, 0:1], WCf[:, l, :], cp[:, l:l + 1], l == 0, l == 31, ["KV1", "cp"], ["PS7"])
                    P.op("dve", lambda e: e.tensor_copy(out=SM[:, 1:2], in_=PS[7][:, 0:1]), reads=["PS7"], writes=["SM"])
                    for g in range(2):
                        for jr in range(4):
                            P.op("sp", lambda e, jr=jr, g=g, row0=row0: e.dma_start(
                                out=KCt[:, 0:8192].rearrange("p (i j q) -> p j i q", i=4, j=4, q=512)[:, jr],
                                in_=GFa[jr, row0 + g * 128: row0 + (g + 1) * 128, :].rearrange("p (i q) -> p i q", q=512)), writes=["KV0"], dma_key="KV0")
                        kv = KCt[:, 0:8192].rearrange("p (n s) -> p n s", s=16)
                        for l in range(32):
                            rhs = kv[:, 0:511, l] if l < 16 else kv[:, 1:512, l - 16]
                            mm(PS[6][:, 0:511], WCb[:, l, :], rhs, l == 0, l == 31, ["KV0", "KV2"], ["PS6"])
                        P.op("dve", lambda e: e.memset(YC[:, 511:512], 0.0), writes=["PC3"])
                        P.op("act", lambda e: e.activation(out=YC[:, 0:511], in_=PS[6][:, 0:511], func=AF.Identity, bias=SM[:, 1:2], scale=1.0), reads=["PS6", "SM"], writes=["PC3"])
                        if kind == "k":
                            P.op("act", lambda e: e.activation(out=SQc[:], in_=YC[:], func=AF.Square), reads=["PC3"], writes=["PT0"])
                            mm(PS[7][:], ones_b[:], SQc[:], True, True, ["PT0", "ones_b"], ["PS7"])
                            P.op("act", lambda e: e.activation(out=RD[:], in_=PS[7][:], func=AF.Sqrt, bias=epsb[:, 0:1], scale=1.0 / 128), reads=["PS7", "epsb"], writes=["RD"])
                            P.op("dve", lambda e: e.reciprocal(out=RD[:], in_=RD[:]), reads=["RD"], writes=["RD"])
                            P.op("dve", lambda e, g=g: e.scalar_tensor_tensor(out=KCN[:, g, :], in0=YC[:], scalar=gk[:, 0:1], in1=RD[:], op0=ALU.mult, op1=ALU.mult),
                                 reads=["PC3", "RD", "gk"], writes=["KCN"])
                        else:
                            for nb in range(4):
                                P.op("pe", lambda e, nb=nb: e.transpose(PS[7][:, nb * 128:(nb + 1) * 128], YC[:, nb * 128:(nb + 1) * 128], IDF[:]), reads=["PC3", "tb_IDF"], writes=["PS7"])
                            P.op("dve", lambda e, g=g: e.tensor_copy(out=VCM[:, g].rearrange("p n d -> p (n d)"), in_=PS[7][:]), reads=["PS7"], writes=["VCM"])
                P.barrier()

        def load_head(qsrc, qrow, krow, vcol, ks, vs, qs):
            if qsrc is not None:
                P.op("sp", lambda e: e.dma_start(out=QT[qs][:], in_=qsrc[qrow:qrow + 128, :]), writes=["QT%d" % qs], dma_key="QT%d" % qs)
            if krow is not None:
                for jr in range(4):
                    P.op("sp", lambda e, jr=jr: e.dma_start(out=KV[ks][:, jr * 2048:(jr + 1) * 2048], in_=GFa[jr, krow:krow + 128, :]), writes=["KV%d" % ks], dma_key="KV%d" % ks)
                    P.op("sp", lambda e, jr=jr: e.dma_start(out=KV[vs][:, jr * 2048:(jr + 1) * 2048].rearrange("p (t d) -> p t d", d=128),
                                                            in_=GTa[jr, :, vcol:vcol + 128].rearrange("(t k) d -> k t d", k=128)), writes=["KV%d" % vs], dma_key="KV%d" % vs)

        cnt = {"ob": 0}

        def attn(i, qs, tiles, Rk, R, m, ob, fp32v=False):
            n = len(tiles)
            q_ap = QT[qs][:, i * 512:(i + 1) * 512]
            Ok, Dk2 = "PS%d" % (2 + ob), "PS%d" % (4 + ob)

            def S(t):
                tl = tiles[t]
                sp_ = C.nxt("sb", 2); sk = "PS%d" % sp_
                last = "qk"
                seq = [("qk", tl["lhsT"], q_ap, [tl["kkey"], "QT%d" % qs]), ("rb", SEL3[:, m, :], R[:], [Rk, "tb_SEL3"])]
                if tl.get("sel") is not None:
                    seq.append(("sel", tl["sel"], NST[:], ["tb_E", "NST"]))
                if tl.get("mask") is not None:
                    seq.append(("mask", IDB[:], tl["mask"], ["tb_IDB", tl["mkey"]]))
                for k_, (nm, l_, r_, rd) in enumerate(seq):
                    mm(PS[sp_][:], l_, r_, k_ == 0, k_ == len(seq) - 1, rd, [sk])
                if fp32v:
                    pt, pk = PC[t], "PC%d" % t
                else:
                    pi = C.nxt("pt", 3); pt, pk = PT[pi], "PT%d" % pi
                P.op("act", lambda e: e.activation(out=pt[:], in_=PS[sp_][:], func=AF.Exp, bias=tl["bias"], scale=1.0), reads=[sk, tl["bkey"]], writes=[pk])
                return pt, pk

            pend = S(0)
            for t in range(n):
                cur = pend
                if t + 1 < n:
                    pend = S(t + 1)
                tl = tiles[t]
                mm(PS[2 + ob][:], tl["v"], cur[0][:], t == 0, t == n - 1, [tl["vkey"], cur[1]], [Ok])
                mm(PS[4 + ob][:], (ones_f if fp32v else ones_b)[:], cur[0][:], t == 0, t == n - 1, [cur[1], "ones_b", "ones_f"], [Dk2])
            P.op("dve", lambda e: e.tensor_scalar(out=RD[:], in0=PS[4 + ob][:], scalar1=1e-30, scalar2=None, op0=ALU.max), reads=[Dk2], writes=["RD"])
            P.op("dve", lambda e: e.reciprocal(out=RD[:], in_=RD[:]), reads=["RD"], writes=["RD"])
            return Ok

        def store_out(row0, i, src_fn, reads):
            o = C.nxt("os", 2); ok = "OS%d" % o
            src_fn(OS[o], ok)
            P.op("sp", lambda e: e.dma_start(out=OT[row0:row0 + 128, i * 512:(i + 1) * 512], in_=OS[o][:]), reads=[ok], dma_key="st_" + ok)

        if "fox" in parts:
            nfox = int(os.environ.get("KNFOX", "8"))
            for h in range(nfox):
                ks, vs, qs = 2 * (h % 2), 2 * (h % 2) + 1, h % 2
                load_head(QF, h * 128, h * 128, h * 128, ks, vs, qs)
                Vv = KV[vs][:].rearrange("p (t d) -> p t d", d=128)
                for i in range(4):
                    tiles = []
                    for kb in range(16 * i + 16):
                        cb = cb_of(kb)
                        jk, ik, u = (kb % 16) // 4, kb // 16, kb % 4
                        pcol = jk * 32 + h * 4 + ik
                        z = kb - 16 * i
                        tiles.append(dict(lhsT=KV[ks][:, cb * 128:(cb + 1) * 128], kkey="KV%d" % ks, bias=DK[:, u, pcol:pcol + 1], bkey="DK",
                                          mask=(MZ[:, z, :] if z >= 0 else None), mkey="tb_MZ", v=Vv[:, cb, :], vkey="KV%d" % vs))
                    ob = C.nxt("ob", 2)
                    Ok = attn(i, qs, tiles, "RF", RF, h * 4 + i, ob)
                    store_out(h * 128, i, lambda dst, dk, ob=ob, Ok=Ok: P.op("dve", lambda e: e.tensor_tensor(out=dst[:], in0=PS[2 + ob][:], in1=RD[:], op=ALU.mult), reads=[Ok, "RD"], writes=[dk]), None)

        if "nsa" in parts:
            def gate_factor(c, i):
                mm(PS[7][:], SELG[:, c, :], GH[:, i * 512:(i + 1) * 512], True, True, ["tb_SELG", "GH"], ["PS7"])
                P.op("dve", lambda e: e.tensor_tensor(out=FT[:], in0=PS[7][:], in1=RD[:], op=ALU.mult), reads=["PS7", "RD"], writes=["FT"])

            for g in range(int(os.environ.get("KNG", "2"))):
                load_head(None, None, 1024 + g * 128, 1024 + g * 128, 0, 1, None)
                load_head(None, None, 1280 + g * 128, 1280 + g * 128, 2, 3, None)
                Vs = KV[1][:].rearrange("p (t d) -> p t d", d=128)
                Vw = KV[3][:].rearrange("p (t d) -> p t d", d=128)
                for i in range(4):
                    for r in range(4):
                        s = 4 * g + r
                        qs = C.nxt("qs", 2)
                        load_head(QN, s * 128, None, None, None, None, qs)
                        tiles = []
                        for nb in range(i + 1):
                            tiles.append(dict(lhsT=KCN[:, g, nb * 128:(nb + 1) * 128], kkey="KCN", bias=KBC[:, s, nb:nb + 1], bkey="tb_KBC",
                                              mask=(MC[:] if nb == i else (MC2[:] if nb == i - 1 else None)), mkey="tb_MC", v=VCM[:, g, nb, :], vkey="VCM"))
                        ob = C.nxt("ob", 2)
                        Ok = attn(i, qs, tiles, "tb_RA", RA, s * 4 + i, ob, fp32v=True)
                        gate_factor(s * 3 + 0, i)
                        P.op("dve", lambda e, r=r, ob=ob: e.tensor_tensor(out=ACC[:, r, :], in0=PS[2 + ob][:], in1=FT[:], op=ALU.mult), reads=[Ok, "FT"], writes=["ACC%d" % r])
                        for u in range(4):
                            for nb in range(i + 1):
                                mm(PS[6][:, u * 128:(u + 1) * 128], PC[nb][:, u * 128:(u + 1) * 128], OV[:, nb, :], nb == 0, nb == i, ["PC%d" % nb, "tb_OV"], ["PS6"])
                        for u in range(4):
                            for nb in range(i + 1):
                                mm(PS[7][:, u:u + 1], PC[nb][:, u * 128:(u + 1) * 128], ones_f[:, 0:1], nb == 0, nb == i, ["PC%d" % nb, "ones_f"], ["PS7"])
                        P.op("dve", lambda e: e.tensor_scalar(out=SM[:, 4:8], in0=PS[7][:, 0:4], scalar1=1e-30, scalar2=None, op0=ALU.max), reads=["PS7"], writes=["SM"])
                        P.op("dve", lambda e: e.reciprocal(out=SM[:, 4:8], in_=SM[:, 4:8]), reads=["SM"], writes=["SM"])
                        for u in range(4):
                            if r == 0:
                                P.op("dve", lambda e, u=u: e.tensor_scalar(out=IMP[:, u, :], in0=PS[6][:, u * 128:(u + 1) * 128], scalar1=SM[:, 4 + u:5 + u], scalar2=None, op0=ALU.mult), reads=["PS6", "SM"], writes=["IMP"])
                            else:
                                P.op("dve", lambda e, u=u: e.scalar_tensor_tensor(out=IMP[:, u, :], in0=PS[6][:, u * 128:(u + 1) * 128], scalar=SM[:, 4 + u:5 + u], in1=IMP[:, u, :], op0=ALU.mult, op1=ALU.add), reads=["PS6", "SM"], writes=["IMP"])
                    P.op("sp", lambda e, i=i: e.dma_start(out=FVMs[:], in_=TD["FVM"][:, i * 4:(i + 1) * 4, :]), writes=["FVMs"], dma_key="FVMs")
                    P.op("sp", lambda e, i=i: e.dma_start(out=FVAs[:], in_=TD["FVA"][:, i * 4:(i + 1) * 4, :]), writes=["FVAs"], dma_key="FVAs")
                    for u in range(4):
                        P.op("dve", lambda e, u=u: e.tensor_tensor(out=W2[:], in0=IMP[:, u, :], in1=FVMs[:, u, :], op=ALU.mult), reads=["IMP", "FVMs"], writes=["W2"])
                        P.op("dve", lambda e, u=u: e.tensor_tensor(out=W2[:], in0=W2[:], in1=FVAs[:, u, :], op=ALU.add), reads=["FVAs"], writes=["W2"])
                        P.op("dve", lambda e: e.max(out=SM[:, 8:16], in_=W2[:]), reads=["W2"], writes=["SM"])
                        P.op("dve", lambda e: e.match_replace(out=W3[:], in_to_replace=SM[:, 8:16], in_values=W2[:], imm_value=-3e9), reads=["W2", "SM"], writes=["W3"])
                        P.op("dve", lambda e: e.max(out=SM[:, 16:24], in_=W3[:]), reads=["W3"], writes=["SM"])
                        P.op("dve", lambda e: e.tensor_scalar(out=W3[:], in0=W2[:], scalar1=SM[:, 23:24], scalar2=NEG, op0=ALU.is_lt, op1=ALU.mult), reads=["W2", "SM"], writes=["W3"])
                        P.op("pe", lambda e: e.transpose(PS[6][:, 0:128], W3[:], IDF[:]), reads=["W3", "tb_IDF"], writes=["PS6"])
                        P.op("dve", lambda e, u=u: e.tensor_copy(out=NST[:, u * 128:(u + 1) * 128], in_=PS[6][:, 0:128]), reads=["PS6"], writes=["NST"])
                    for br in ("slc", "win"):
                        for r in range(4):
                            s = 4 * g + r
                            qs = C.nxt("qs", 2)
                            load_head(QN, s * 128, None, None, None, None, qs)
                            tiles = []
                            if br == "slc":
                                for kb in range(16 * i + 16):
                                    cb = cb_of(kb); z = kb - 16 * i
                                    tiles.append(dict(lhsT=KV[0][:, cb * 128:(cb + 1) * 128], kkey="KV0", bias=KBA[:, s, cb:cb + 1], bkey="tb_KBA",
                                                      mask=(MZ[:, z, :] if z >= 0 else None), mkey="tb_MZ", sel=E[:, cb, :], v=Vs[:, cb, :], vkey="KV1"))
                            else:
                                for z in range(20):
                                    kb = 16 * i - 4 + z
                                    if kb < 0:
                                        continue
                                    cb = cb_of(kb)
                                    tiles.append(dict(lhsT=KV[2][:, cb * 128:(cb + 1) * 128], kkey="KV2", bias=KBA[:, s, cb:cb + 1], bkey="tb_KBA",
                                                      mask=MW[:, z, :], mkey="tb_MW", v=Vw[:, cb, :], vkey="KV3"))
                            ob = C.nxt("ob", 2)
                            Ok = attn(i, qs, tiles, "tb_RA", RA, s * 4 + i, ob)
                            gate_factor(s * 3 + (1 if br == "slc" else 2), i)
                            P.op("dve", lambda e, ob=ob: e.tensor_tensor(out=TMP[:], in0=PS[2 + ob][:], in1=FT[:], op=ALU.mult), reads=[Ok, "FT"], writes=["TMP"])
                            if br == "slc":
                                P.op("dve", lambda e, r=r: e.tensor_tensor(out=ACC[:, r, :], in0=ACC[:, r, :], in1=TMP[:], op=ALU.add), reads=["TMP"], writes=["ACC%d" % r])
                            else:
                                store_out((8 + s) * 128, i, lambda dst, dk, r=r: P.op("dve", lambda e: e.tensor_tensor(out=dst[:], in0=ACC[:, r, :], in1=TMP[:], op=ALU.add), reads=["ACC%d" % r, "TMP"], writes=[dk]), None)
        print("B ops", P.emit())
    return nc


def build_C():
    nc = bass.Bass("TRN2", target_bir_lowering=False)
    C = Ctx(nc)
    P = C.P
    OTd = C.din("OT", [2048, TO], BF16)
    xT = C.din("xT", [D, TO], F32)
    w_out = C.din("w_out", [D, D], F32)
    g2_d = C.din("ffn_g", [128, 16], F32)
    w_gate = C.din("w_gate", [D, DFF], F32)
    w_up = C.din("w_up", [D, DFF], F32)
    w_down = C.din("w_down", [DFF, D], F32)
    X1 = C.dout("X1T", [D, TO], F32)
    X2 = C.dout("X2T", [D, TO], F32)
    NF = DFF // 128
    with C.es:
        HT = C.sb("HT", [128, 16, TO], BF16)
        AT = C.sb("AT", [128, NF, 512], BF16)
        WA = [C.sb("WA%d" % k, [128, 16, 256], BF16) for k in range(4)]
        WD = [C.sb("WD%d" % k, [128, NF, 128], BF16) for k in range(2)]
        XP = [C.sb("XP%d" % k, [128, 512], F32) for k in range(3)]
        XO = [C.sb("XO%d" % k, [128, 512], F32) for k in range(3)]
        SQ = [C.sb("SQ%d" % k, [128, 512], BF16) for k in range(2)]
        SG = [C.sb("SG%d" % k, [128, 512], F32) for k in range(2)]
        RS2 = C.sb("RS2", [128, 4, 512], F32)
        g2 = C.sb("g2", [128, 16], F32)
        ones = C.sb("ones", [128, 128], BF16)
        epsb = C.sb("epsb", [128, 1], F32)
        PS = [C.es.enter_context(nc.psum_tensor("PS%d" % k, [128, 512], F32)) for k in range(8)]
        P.op("sp", lambda e: e.dma_start(out=g2[:], in_=g2_d), writes=["g2"], dma_key="g2")
        P.op("dve", lambda e: e.memset(ones[:], 1.0), writes=["ones"])
        P.op("dve", lambda e: e.memset(epsb[:], EPS), writes=["epsb"])
        ov = OTd.rearrange("(dc p) t -> p dc t", p=128)
        for i in range(4):
            P.op("sp", lambda e, i=i: e.dma_start(out=HT[:, :, i * 512:(i + 1) * 512], in_=ov[:, :, i * 512:(i + 1) * 512]), writes=["HT%d" % i], dma_key="HT%d" % i)
        wov = w_out.rearrange("(dc p) c -> p dc c", p=128)
        for og in range(8):
            ws = C.nxt("wa", 4)
            P.op("pool", lambda e, ws=ws, og=og: e.dma_start(out=WA[ws][:], in_=wov[:, :, og * 256:(og + 1) * 256]), writes=["WA%d" % ws], dma_key="WA%d" % ws)
            for oc in range(2):
                ob = og * 2 + oc
                for i in range(4):
                    p = C.nxt("ps", 4); pk = "PS%d" % p
                    for dc in range(16):
                        P.op("pe", lambda e, dc=dc, ws=ws, oc=oc, i=i, p=p: e.matmul(PS[p][:], lhsT=WA[ws][:, dc, oc * 128:(oc + 1) * 128], rhs=HT[:, dc, i * 512:(i + 1) * 512], start=(dc == 0), stop=(dc == 15)),
                             reads=["WA%d" % ws, "HT%d" % i], writes=[pk])
                    xp = C.nxt("xp", 3); xo = C.nxt("xo", 3); s = C.nxt("sq", 2)
                    P.op("sp", lambda e, xp=xp, ob=ob, i=i: e.dma_start(out=XP[xp][:], in_=xT[ob * 128:(ob + 1) * 128, i * 512:(i + 1) * 512]), writes=["XP%d" % xp], dma_key="XP%d" % xp)
                    P.op("dve", lambda e, xp=xp, xo=xo, p=p: e.tensor_tensor(out=XO[xo][:], in0=PS[p][:], in1=XP[xp][:], op=ALU.add), reads=[pk, "XP%d" % xp], writes=["XO%d" % xo])
                    P.op("sp", lambda e, xo=xo, ob=ob, i=i: e.dma_start(out=X1[ob * 128:(ob + 1) * 128, i * 512:(i + 1) * 512], in_=XO[xo][:]), reads=["XO%d" % xo], writes=["x1d_%d_%d" % (ob, i)], dma_key="st_XO%d" % xo)
                    P.op("act", lambda e, xo=xo, s=s: e.activation(out=SQ[s][:], in_=XO[xo][:], func=AF.Square), reads=["XO%d" % xo], writes=["SQ%d" % s])
                    P.op("pe", lambda e, s=s, i=i, ob=ob: e.matmul(PS[4 + i][:], lhsT=ones[:], rhs=SQ[s][:], start=(ob == 0), stop=(ob == 15)), reads=["SQ%d" % s, "ones"], writes=["PS%d" % (4 + i)])
        for i in range(4):
            P.op("act", lambda e, i=i: e.activation(out=RS2[:, i, :], in_=PS[4 + i][:], func=AF.Sqrt, bias=epsb[:, 0:1], scale=1.0 / D), reads=["PS%d" % (4 + i), "epsb"], writes=["RS2_%d" % i])
            P.op("dve", lambda e, i=i: e.reciprocal(out=RS2[:, i, :], in_=RS2[:, i, :]), reads=["RS2_%d" % i], writes=["RS2_%d" % i])
        for i in range(4):
            for ob in range(16):
                xp = C.nxt("xp", 3)
                P.op("sp", lambda e, xp=xp, ob=ob, i=i: e.dma_start(out=XP[xp][:], in_=X1[ob * 128:(ob + 1) * 128, i * 512:(i + 1) * 512]), reads=["x1d_%d_%d" % (ob, i)], writes=["XP%d" % xp], dma_key="XP%d" % xp)
                P.op("dve", lambda e, xp=xp, ob=ob, i=i: e.scalar_tensor_tensor(out=HT[:, ob, i * 512:(i + 1) * 512], in0=XP[xp][:], scalar=g2[:, ob:ob + 1], in1=RS2[:, i, :], op0=ALU.mult, op1=ALU.mult),
                     reads=["XP%d" % xp, "g2", "RS2_%d" % i], writes=["HT%d" % i])
        wgv = w_gate.rearrange("(dc p) c -> p dc c", p=128)
        wuv = w_up.rearrange("(dc p) c -> p dc c", p=128)
        wdv = w_down.rearrange("(fc p) c -> p fc c", p=128)
        for i in range(4):
            for fg in range(NF // 2):
                wsg = C.nxt("wa", 4); wsu = C.nxt("wa", 4)
                P.op("pool", lambda e, wsg=wsg, fg=fg: e.dma_start(out=WA[wsg][:], in_=wgv[:, :, fg * 256:(fg + 1) * 256]), writes=["WA%d" % wsg], dma_key="WA%d" % wsg)
                P.op("pool", lambda e, wsu=wsu, fg=fg: e.dma_start(out=WA[wsu][:], in_=wuv[:, :, fg * 256:(fg + 1) * 256]), writes=["WA%d" % wsu], dma_key="WA%d" % wsu)
                for fc in range(2):
                    f = fg * 2 + fc
                    pg = C.nxt("ps", 4); pu = C.nxt("ps", 4)
                    for (pp, wsx) in ((pg, wsg), (pu, wsu)):
                        for dc in range(16):
                            P.op("pe", lambda e, dc=dc, pp=pp, wsx=wsx, fc=fc, i=i: e.matmul(PS[pp][:], lhsT=WA[wsx][:, dc, fc * 128:(fc + 1) * 128], rhs=HT[:, dc, i * 512:(i + 1) * 512], start=(dc == 0), stop=(dc == 15)),
                                 reads=["WA%d" % wsx, "HT%d" % i], writes=["PS%d" % pp])
                    sg = C.nxt("sg", 2)
                    P.op("act", lambda e, sg=sg, pg=pg: e.activation(out=SG[sg][:], in_=PS[pg][:], func=AF.Silu), reads=["PS%d" % pg], writes=["SG%d" % sg])
                    P.op("dve", lambda e, sg=sg, pu=pu, f=f: e.tensor_tensor(out=AT[:, f, :], in0=PS[pu][:], in1=SG[sg][:], op=ALU.mult), reads=["PS%d" % pu, "SG%d" % sg], writes=["AT"])
            for ob in range(16):
                wd = C.nxt("wd", 2)
                for hf in range(2):
                    P.op("pool", lambda e, wd=wd, ob=ob, hf=hf: e.dma_start(out=WD[wd][:, hf * 22:(hf + 1) * 22, :], in_=wdv[:, hf * 22:(hf + 1) * 22, ob * 128:(ob + 1) * 128]), writes=["WD%d" % wd], dma_key="WD%d" % wd)
                p = C.nxt("ps", 4); pk = "PS%d" % p
                for f in range(NF):
                    P.op("pe", lambda e, f=f, wd=wd, p=p: e.matmul(PS[p][:], lhsT=WD[wd][:, f, :], rhs=AT[:, f, :], start=(f == 0), stop=(f == NF - 1)), reads=["WD%d" % wd, "AT"], writes=[pk])
                xp = C.nxt("xp", 3); xo = C.nxt("xo", 3)
                P.op("sp", lambda e, xp=xp, ob=ob, i=i: e.dma_start(out=XP[xp][:], in_=X1[ob * 128:(ob + 1) * 128, i * 512:(i + 1) * 512]), reads=["x1d_%d_%d" % (ob, i)], writes=["XP%d" % xp], dma_key="XP%d" % xp)
                P.op("dve", lambda e, xp=xp, xo=xo, p=p: e.tensor_tensor(out=XO[xo][:], in0=PS[p][:], in1=XP[xp][:], op=ALU.add), reads=[pk, "XP%d" % xp], writes=["XO%d" % xo])
                P.op("sp", lambda e, xo=xo, ob=ob, i=i: e.dma_start(out=X2[ob * 128:(ob + 1) * 128, i * 512:(i + 1) * 512], in_=XO[xo][:]), reads=["XO%d" % xo], dma_key="st_XO%d" % xo)
        print("C ops", P.emit())
    return nc


def _own_tokens(j):
    return np.concatenate([np.arange(512 * (4 * i + j), 512 * (4 * i + j) + 512) for i in range(4)])


_NC = {}


def _prog(name):
    if name not in _NC:
        _NC[name] = {"A": build_A, "B": build_B, "C": build_C}[name]()
    return _NC[name]


def kernel(**inputs):
    p = {k: np.asarray(v) for k, v in inputs.items()}
    x = p["x"]
    cores = list(range(8))
    xT = [np.ascontiguousarray(x[c // 4][_own_tokens(c % 4)].T) for c in cores]
    for l in range(2):
        hg = np.zeros((128, 8), np.float32)
        for k, nm in enumerate(["fox_q_norm", "fox_k_norm", "nsa_q_norm", "slc_k_norm", "win_k_norm"]):
            hg[:, k] = p[nm][l]
        w_in = np.ascontiguousarray(p["w_in"][l])
        attn_g = np.ascontiguousarray(p["attn_norm"][l].reshape(16, 128).T)
        fbias = np.ascontiguousarray(p["fox_forget_bias"][l].reshape(8, 1))
        ra = run_bass_kernel_spmd(_prog("A"), [{"xT": xT[c], "w_in": w_in, "attn_g": attn_g, "head_g": hg, "fbias": fbias} for c in cores],
                                  core_ids=cores).results
        common = {"cwk": np.ascontiguousarray(p["cmp_w_k"][l]), "cwv": np.ascontiguousarray(p["cmp_w_v"][l]),
                  "cpkT": np.ascontiguousarray(p["cmp_pos_k"][l].T), "cpvT": np.ascontiguousarray(p["cmp_pos_v"][l].T),
                  "ckn": np.ascontiguousarray(p["cmp_k_norm"][l].reshape(128, 1))}
        gath = {}
        for b in range(2):
            gath[b] = {"GFa": np.stack([np.asarray(ra[4 * b + r]["GF"]) for r in range(4)]),
                       "GTa": np.stack([np.asarray(ra[4 * b + r]["GT"]) for r in range(4)]),
                       "GLa": np.stack([np.asarray(ra[4 * b + r]["GL"]) for r in range(4)])}
        bin_ = []
        for c in cores:
            m = {"QF": np.asarray(ra[c]["QF"]), "QN": np.asarray(ra[c]["QN"]), "GATES": np.asarray(ra[c]["GATES"])}
            m.update(gath[c // 4])
            m.update(common)
            for nm, v in tables(c % 4).items():
                m["T_" + nm] = v
            bin_.append(m)
        rb = run_bass_kernel_spmd(_prog("B"), bin_, core_ids=cores).results
        cw = {"w_out": np.ascontiguousarray(p["w_out"][l]), "ffn_g": np.ascontiguousarray(p["ffn_norm"][l].reshape(16, 128).T),
              "w_gate": np.ascontiguousarray(p["w_gate"][l]), "w_up": np.ascontiguousarray(p["w_up"][l]),
              "w_down": np.ascontiguousarray(p["w_down"][l])}
        rc = run_bass_kernel_spmd(_prog("C"), [dict(cw, OT=np.asarray(rb[c]["OT"]), xT=xT[c]) for c in cores], core_ids=cores).results
        xT = [np.ascontiguousarray(np.asarray(rc[c]["X2T"])) for c in cores]
    out = np.empty_like(x)
    for c in cores:
        out[c // 4][_own_tokens(c % 4)] = xT[c].T
    return out
```
